# Optimizing a Trainium2 kernel written in Bass

```python
import math
import jax, jax.numpy as jnp
from jax import lax
import numpy as np

D_MODEL = 2048
BATCH = 4
SEQ = 4096
DEPTH = 2

CHUNK = 64
N_EVEN = (DEPTH + 1) // 2
N_ODD = DEPTH // 2
EPS = 1e-6

CONV_DIM = D_MODEL
CONV_WIDTH = 31
GMLP_DIM = D_MODEL
GMLP_GROUPS = 8
GMLP_GROUP_DIM = GMLP_DIM // GMLP_GROUPS
GMLP_CHUNK = 128
EVEN_IN = 2 * CONV_DIM + 2 * GMLP_DIM
EVEN_OUT = CONV_DIM + GMLP_DIM
DIFF_HEADS = 8
DIFF_HEAD_DIM = D_MODEL // (2 * DIFF_HEADS)
DIFF_V_DIM = 2 * DIFF_HEAD_DIM
Q_BLOCK = 128
ODD_IN = 3 * D_MODEL
FFN_HIDDEN = ((8 * D_MODEL + 3 * 256 - 1) // (3 * 256)) * 256

kernel_name = "hybrid_conv_gmlp_diffattn_encoder"


def rms_norm(x, g):
    xf = x.astype(jnp.float32)
    y = xf * lax.rsqrt(jnp.mean(xf * xf, axis=-1, keepdims=True) + EPS)
    return (y * g.astype(jnp.float32)).astype(x.dtype)


def layer_norm(x, g, b):
    xf = x.astype(jnp.float32)
    mu = jnp.mean(xf, axis=-1, keepdims=True)
    var = jnp.mean(jnp.square(xf - mu), axis=-1, keepdims=True)
    y = (xf - mu) * lax.rsqrt(var + EPS) * g.astype(jnp.float32) + b.astype(jnp.float32)
    return y.astype(x.dtype)


def conformer_conv(a_val, a_gate, w_dw, b_dw, ln_g, ln_b):
    h = a_val * jax.nn.sigmoid(a_gate)
    h = lax.conv_general_dilated(
        h, w_dw[:, None, :], window_strides=(1,),
        padding=[(CONV_WIDTH - 1, 0)],
        dimension_numbers=("NWC", "WIO", "NWC"),
        feature_group_count=CONV_DIM) + b_dw
    h = layer_norm(h, ln_g, ln_b)
    return jax.nn.silu(h)


def chunked_spatial_gating(u, v, w_s, b_s, ln_g, ln_b):
    B, S, _ = u.shape
    v = layer_norm(v, ln_g, ln_b)
    v = v.reshape(B, S // GMLP_CHUNK, GMLP_CHUNK, GMLP_GROUPS, GMLP_GROUP_DIM)
    pos = jnp.arange(GMLP_CHUNK)
    mask = (pos[None, :] // CHUNK) <= (pos[:, None] // CHUNK)
    w = jnp.where(mask[None], w_s, jnp.zeros_like(w_s))
    s = jnp.einsum("gij,bcjgd->bcigd", w, v) + b_s.T[None, None, :, :, None]
    return u * s.reshape(B, S, GMLP_DIM)


def diff_attention(h, w_qkv, w_o, lq1, lk1, lq2, lk2, subln_g, lambda_init):
    B, S, _ = h.shape
    qkv = h @ w_qkv
    q, k, v = jnp.split(qkv, 3, axis=-1)
    q = q.reshape(B, S, DIFF_HEADS, 2, DIFF_HEAD_DIM)
    k = k.reshape(B, S, DIFF_HEADS, 2, DIFF_HEAD_DIM)
    v = v.reshape(B, S, DIFF_HEADS, DIFF_V_DIM)
    f32 = jnp.float32
    lam = (jnp.exp(jnp.sum(lq1.astype(f32) * lk1.astype(f32)))
           - jnp.exp(jnp.sum(lq2.astype(f32) * lk2.astype(f32))) + lambda_init)
    scale = DIFF_HEAD_DIM ** -0.5
    nb = S // Q_BLOCK
    q_blocks = q.reshape(B, nb, Q_BLOCK, DIFF_HEADS, 2, DIFF_HEAD_DIM).transpose(1, 0, 2, 3, 4, 5)
    key_chunk = jnp.arange(S) // CHUNK

    def block(args):
        q_blk, idx = args
        scores = jnp.einsum("bqhcd,bkhcd->bhcqk", q_blk, k).astype(f32) * scale
        q_chunk = (idx * Q_BLOCK + jnp.arange(Q_BLOCK)) // CHUNK
        mask = key_chunk[None, :] <= q_chunk[:, None]
        scores = jnp.where(mask, scores, -jnp.inf)
        p = jax.nn.softmax(scores, axis=-1)
        attn = p[:, :, 0] - lam * p[:, :, 1]
        return jnp.einsum("bhqk,bkhe->bqhe", attn.astype(v.dtype), v)

    o = lax.map(block, (q_blocks, jnp.arange(nb)))
    o = o.transpose(1, 0, 2, 3, 4).reshape(B, S, DIFF_HEADS, DIFF_V_DIM)
    o = rms_norm(o, subln_g) * (1.0 - lambda_init)
    return o.reshape(B, S, D_MODEL) @ w_o


def swiglu(h, w_gate_up, w_down):
    g, u = jnp.split(h @ w_gate_up, 2, axis=-1)
    return (jax.nn.silu(g) * u) @ w_down


def setup_inputs(seed: int = 0) -> dict:
    key = jax.random.key(seed)
    ks = jax.random.split(key, 24)
    f32 = jnp.float32
    nrm = lambda k, shape, s: (jax.random.normal(k, shape, f32) * s)
    gain = lambda k, shape: 1.0 + 0.05 * jax.random.normal(k, shape, f32)
    return {
        "x": jax.random.normal(ks[0], (BATCH, SEQ, D_MODEL), f32),
        "w_in_even": nrm(ks[1], (N_EVEN, D_MODEL, EVEN_IN), D_MODEL ** -0.5),
        "conv_w": nrm(ks[2], (N_EVEN, CONV_WIDTH, CONV_DIM), CONV_WIDTH ** -0.5),
        "conv_b": nrm(ks[3], (N_EVEN, CONV_DIM), 0.01),
        "conv_ln_g": gain(ks[4], (N_EVEN, CONV_DIM)),
        "conv_ln_b": nrm(ks[5], (N_EVEN, CONV_DIM), 0.01),
        "gmlp_ln_g": gain(ks[6], (N_EVEN, GMLP_DIM)),
        "gmlp_ln_b": nrm(ks[7], (N_EVEN, GMLP_DIM), 0.01),
        "gmlp_w_s": nrm(ks[8], (N_EVEN, GMLP_GROUPS, GMLP_CHUNK, GMLP_CHUNK), GMLP_CHUNK ** -0.5),
        "gmlp_b_s": 1.0 + nrm(ks[9], (N_EVEN, GMLP_GROUPS, GMLP_CHUNK), 0.01),
        "w_out_even": nrm(ks[10], (N_EVEN, EVEN_OUT, D_MODEL), EVEN_OUT ** -0.5),
        "w_qkv_odd": nrm(ks[11], (N_ODD, D_MODEL, ODD_IN), D_MODEL ** -0.5),
        "w_o_odd": nrm(ks[12], (N_ODD, D_MODEL, D_MODEL), D_MODEL ** -0.5),
        "lambda_q1": nrm(ks[13], (N_ODD, DIFF_HEAD_DIM), 0.1),
        "lambda_k1": nrm(ks[14], (N_ODD, DIFF_HEAD_DIM), 0.1),
        "lambda_q2": nrm(ks[15], (N_ODD, DIFF_HEAD_DIM), 0.1),
        "lambda_k2": nrm(ks[16], (N_ODD, DIFF_HEAD_DIM), 0.1),
        "subln_g": gain(ks[17], (N_ODD, DIFF_V_DIM)),
        "mix_norm_g": gain(ks[18], (DEPTH, D_MODEL)),
        "ffn_norm_g": gain(ks[19], (DEPTH, D_MODEL)),
        "w_gate_up": nrm(ks[20], (DEPTH, D_MODEL, 2 * FFN_HIDDEN), D_MODEL ** -0.5),
        "w_down": nrm(ks[21], (DEPTH, FFN_HIDDEN, D_MODEL), FFN_HIDDEN ** -0.5),
        "final_norm_g": gain(ks[22], (D_MODEL,)),
    }


def reference(x, w_in_even, conv_w, conv_b, conv_ln_g, conv_ln_b, gmlp_ln_g, gmlp_ln_b,
              gmlp_w_s, gmlp_b_s, w_out_even, w_qkv_odd, w_o_odd, lambda_q1, lambda_k1,
              lambda_q2, lambda_k2, subln_g, mix_norm_g, ffn_norm_g, w_gate_up, w_down,
              final_norm_g):
    for layer in range(DEPTH):
        h = rms_norm(x, mix_norm_g[layer])
        if layer % 2 == 0:
            i = layer // 2
            z = h @ w_in_even[i]
            a_val, a_gate, b_u, b_v = jnp.split(
                z, [CONV_DIM, 2 * CONV_DIM, 2 * CONV_DIM + GMLP_DIM], axis=-1)
            a = conformer_conv(a_val, a_gate, conv_w[i], conv_b[i], conv_ln_g[i], conv_ln_b[i])
            b = chunked_spatial_gating(jax.nn.gelu(b_u, approximate=False),
                                       jax.nn.gelu(b_v, approximate=False),
                                       gmlp_w_s[i], gmlp_b_s[i], gmlp_ln_g[i], gmlp_ln_b[i])
            y = jnp.concatenate([a, b], axis=-1) @ w_out_even[i]
        else:
            i = layer // 2
            lambda_init = 0.8 - 0.6 * math.exp(-0.3 * layer)
            y = diff_attention(h, w_qkv_odd[i], w_o_odd[i], lambda_q1[i], lambda_k1[i],
                               lambda_q2[i], lambda_k2[i], subln_g[i], lambda_init)
        x = x + y
        x = x + swiglu(rms_norm(x, ffn_norm_g[layer]), w_gate_up[layer], w_down[layer])
    return rms_norm(x, final_norm_g)
```

```python
import contextlib
import math
import numpy as np
import concourse.bass as bass
import concourse.mybir as mybir
from concourse.bass_utils import run_bass_kernel_spmd

F32 = mybir.dt.float32
BF16 = mybir.dt.bfloat16
AF = mybir.ActivationFunctionType
ALU = mybir.AluOpType

D = 2048
KC = 16
T = 512
NTOK = 2048
NT = NTOK // T
FH = 5632
FC = FH // 128
EPS = 1e-6
HALO = 32
NCORES = 8


class Tk:
    __slots__ = ("name", "lw", "rd")

    def __init__(self, name):
        self.name = name
        self.lw = None
        self.rd = {}


class Op:
    __slots__ = ("eng", "fn", "deps", "dma", "sem", "semval", "sig", "val", "idx")


ENGS = ("pe", "act", "dve", "pool", "sp")
N_DMA_SEMS = 40


class Sched:
    def __init__(self, nc, stack):
        self.nc = nc
        self.ops = {e: [] for e in ENGS}
        self.allops = []
        self.esem = {e: stack.enter_context(nc.semaphore("s_" + e)) for e in ENGS}
        self.dsem = [stack.enter_context(nc.semaphore("d%d" % i)) for i in range(N_DMA_SEMS)]
        self.dcount = [0] * N_DMA_SEMS
        self.dlast = [None] * N_DMA_SEMS
        self.dnext = 0
        self.pending_dma = []

    def op(self, eng, fn, reads=(), writes=(), dma=False, extra=()):
        o = Op()
        o.eng = eng
        o.fn = fn
        o.dma = dma
        o.sig = False
        o.val = None
        deps = []
        for t in reads:
            if t.lw is not None:
                deps.append(t.lw)
        for t in writes:
            if t.lw is not None:
                deps.append(t.lw)
            deps.extend(t.rd.values())
        deps.extend(extra)
        if dma:
            s = self.dnext
            self.dnext = (self.dnext + 1) % N_DMA_SEMS
            if self.dlast[s] is not None:
                deps.append(self.dlast[s])
            self.dcount[s] += 1
            o.sem = s
            o.semval = 16 * self.dcount[s]
            self.dlast[s] = o
        seen = set()
        dd = []
        for d in deps:
            if id(d) not in seen and d is not o:
                seen.add(id(d))
                dd.append(d)
        o.deps = dd
        for d in dd:
            if not d.dma:
                if d.eng == "pe" and eng == "pe" and not dma:
                    continue
                d.sig = True
        for t in reads:
            t.rd[("dma", id(o)) if dma else eng] = o
        for t in writes:
            t.lw = o
            t.rd = {}
        o.idx = len(self.ops[eng])
        self.ops[eng].append(o)
        self.allops.append(o)
        return o

    def barrier(self):
        lasts = [self.ops[e][-1] for e in ENGS if self.ops[e]]
        pend = [d for d in self.dlast if d is not None]
        self.pending_dma = []
        for e in ENGS:
            self.op(e, None, extra=lasts + pend)

    def emit(self):
        nc = self.nc
        for e in ENGS:
            c = 0
            for o in self.ops[e]:
                if o.sig and not o.dma:
                    c += 1
                    o.val = c
            assert c < 60000, (e, c)

        def run(ename, eng):
            waited = {}
            for o in self.ops[ename]:
                for d in o.deps:
                    if d.dma:
                        key = ("d", d.sem)
                        v = d.semval
                        sem = self.dsem[d.sem]
                    else:
                        if d.eng == "pe" and ename == "pe" and not o.dma:
                            continue
                        key = ("e", d.eng)
                        v = d.val
                        sem = self.esem[d.eng]
                    if waited.get(key, 0) >= v:
                        continue
                    waited[key] = v
                    eng.wait_ge(sem, v)
                if o.fn is None:
                    if o.sig:
                        eng.nop().then_inc(self.esem[ename], 1)
                    continue
                ins = o.fn(eng)
                if o.dma:
                    ins.then_inc(self.dsem[o.sem], 16)
                elif o.sig:
                    ins.then_inc(self.esem[ename], 1)

        with nc.Block() as block:
            @block.tensor
            def _(e):
                run("pe", e)

            @block.scalar
            def _(e):
                run("act", e)

            @block.vector
            def _(e):
                run("dve", e)

            @block.gpsimd
            def _(e):
                run("pool", e)

            @block.sync
            def _(e):
                run("sp", e)


class Builder:
    def __init__(self, cfg):
        self.cfg = cfg
        self.nc = bass.Bass("TRN2", target_bir_lowering=False)
        self.stack = contextlib.ExitStack()
        self.S = None
        self.rr = {}

    def din(self, name, shape, dt=F32):
        return self.nc.dram_tensor(name, list(shape), dt, kind="ExternalInput").ap()

    def dout(self, name, shape, dt=F32):
        return self.nc.dram_tensor(name, list(shape), dt, kind="ExternalOutput").ap()

    def dint(self, name, shape, dt=F32):
        return self.nc.dram_tensor(name, list(shape), dt, kind="Internal").ap()

    def sb(self, name, shape, dt):
        return self.stack.enter_context(self.nc.sbuf_tensor(name, list(shape), dt))

    def ps(self, name, shape, dt):
        return self.stack.enter_context(self.nc.psum_tensor(name, list(shape), dt))


ARENA_BYTES = 204 * 1024
SCALE = 128 ** -0.5
LAMBDA_INIT = 0.8 - 0.6 * math.exp(-0.3 * 1)
NEG = -30000.0


def _dsize(dt):
    return 4 if dt == F32 else 2


class Prog(Builder):
    def carve(self, shape, dt):
        n = 1
        for v in shape[1:]:
            n *= v
        nbytes = n * _dsize(dt)
        off = self.cur
        self.cur += (nbytes + 31) // 32 * 32
        assert self.cur <= ARENA_BYTES, ("arena overflow", self.cur)
        ap = self.arena[:, off // 2:(off + nbytes) // 2]
        if dt != BF16:
            ap = ap.bitcast(dt)
        if len(shape) == 3:
            ap = ap.rearrange("p (a b) -> p a b", a=shape[1])
        elif len(shape) == 4:
            ap = ap.rearrange("p (a b c) -> p a b c", a=shape[1], b=shape[2])
        return ap

    def nxt(self, key, n):
        v = self.rr.get(key, 0)
        self.rr[key] = (v + 1) % n
        return v

    def bank(self, lo=0, hi=8):
        i = lo + self.nxt(("bank", lo, hi), hi - lo)
        return self.banks[i], self.bank_tk[i]

    def build(self):
        nc = self.nc
        st = self.stack
        self.S = Sched(nc, st)
        mode = self.cfg["mode"]
        self.mode = mode
        self.col = self.cfg["col"]
        self.declare_io()
        self.arena = self.sb("arena", [128, ARENA_BYTES // 2], BF16)[:, :]
        self.banks = [self.ps("bank%d" % i, [128, 512], F32)[:, :] for i in range(8)]
        self.bank_tk = [Tk("bank%d" % i) for i in range(8)]
        self.cur = 0
        self.alloc_persistent()
        self.consts()
        mark = self.cur
        if "A" in mode:
            self.alloc_main(phase_a=True)
            self.setup_a()
            for tt in range(self.cfg.get("ntiles", NT)):
                self.layer0_tile(tt)
            self.S.barrier()
        if mode == "AB":
            self.exchange()
            self.S.barrier()
        if "B" in mode:
            self.cur = mark
            self.alloc_b1()
            self.setup_b()
            self.attention()
            self.S.barrier()
            self.cur = mark
            self.alloc_main(phase_a=False)
            for tt in range(self.cfg.get("ntiles", NT)):
                self.layer1_tile(tt)
        self.S.barrier()
        self.S.emit()
        st.close()
        return nc

    def declare_io(self):
        c = self.cfg
        m = self.mode
        self.pcol = self.din("pcol", [128, c["npcol"]])
        self.w_gate_up = self.din("w_gate_up", [2, D, 2 * FH])
        self.w_down = self.din("w_down", [2, FH, D])
        if "A" in m:
            self.xin = self.din("xin", [HALO + NTOK, D])
            self.w_in = self.din("w_in", [D, 8192])
            self.w_out = self.din("w_out", [4096, D])
            self.w_qkv = self.din("w_qkv", [D, 6144])
            self.wsT = self.din("wsT", [128, 8 * 128])
            self.bs = self.din("bs", [1, 8 * 128])
        if "B" in m:
            self.w_o = self.din("w_o", [D, D])
            self.lamv = self.din("lamv", [4, 128])
            self.rbias_in = self.din("rbias", [128, 1])
            self.fng = self.din("fng", [1, D])
            self.out = self.dout("out", [NTOK, D])
        mk_out = self.dout if m == "A" else (self.din if m == "B" else self.dint)
        self.x1s = mk_out("x1s", [NTOK, D], F32)
        self.qTs = mk_out("qTs", [16, 128, NTOK], BF16)
        self.kTs = mk_out("kTs", [16, 128, NTOK], BF16)
        self.vs = mk_out("vs", [NTOK, D], BF16)
        if m == "B":
            self.kTr = self.din("kTr", [16, 128, NTOK], BF16)
            self.vr = self.din("vr", [NTOK, D], BF16)
            self.oTs = (self.dout if self.cfg.get("debug") else self.dint)("oTs", [16, 128, NTOK], BF16)
        elif m == "AB":
            self.kTr = self.dint("kTr", [16, 128, NTOK], BF16)
            self.vr = self.dint("vr", [NTOK, D], BF16)
            self.oTs = self.dint("oTs", [16, 128, NTOK], BF16)

    def alloc_persistent(self):
        cv = self.carve
        self.pc = cv([128, self.cfg["npcol"]], F32)
        self.pc_tk = Tk("pc")
        self.stat = cv([128, 64], F32)
        self.stat_tk = Tk("stat")
        self.ident = cv([128, 128], BF16)
        self.ident_f = cv([128, 128], F32)
        self.ones = cv([128, 128], BF16)
        self.ident_tk = Tk("ident")
        self.misc = cv([128, 16], F32)
        self.misc_tk = Tk("misc")

    def alloc_main(self, phase_a):
        cv = self.carve
        self.x_sb = cv([128, 4, D], F32)
        self.x_tk = [Tk("x%d" % i) for i in range(4)]
        self.hT = cv([128, KC, T], BF16)
        self.hT_tk = [Tk("hT%d" % i) for i in range(4)]
        self.region = cv([128, 48, T], BF16)
        self.reg_tk = [Tk("reg%d" % i) for i in range(48)]
        self.xn = [cv([128, D], BF16)]
        self.xn_tk = [Tk("xn0")]
        NW = 3
        self.wfm = [cv([128, KC, 256], BF16) for i in range(NW)]
        self.wfm_tk = [Tk("wfm%d" % i) for i in range(NW)]
        self.wtm = [cv([128, 4, 512], BF16) for i in range(3)]
        self.wtm_tk = [Tk("wtm%d" % i) for i in range(3)]
        self.sg = [cv([128, T], F32) for i in range(2)]
        self.sg_tk = [Tk("sg%d" % i) for i in range(2)]
        if phase_a:
            self.gluT = cv([128, 16, HALO + T], BF16)
            self.glu_tk = [Tk("glu%d" % i) for i in range(16)]
            self.dg = [cv([128, 31, 128], BF16) for i in range(2)]
            self.dg_tk = [Tk("dg%d" % i) for i in range(2)]
            self.Qg = cv([128, 16, 128], F32)
            self.Qg_tk = Tk("Qg")
            self.WT = cv([128, 8, 128], BF16)
            self.WT_tk = Tk("WT")
            self.hTh = cv([128, 16, HALO], BF16)
            self.hTh_tk = Tk("hTh")
            self.cst = cv([128, 3, T], F32)
            self.cst_tk = Tk("cst")
            self.tmpq = [cv([128, 128], F32) for i in range(2)]
            self.tmpq_tk = [Tk("tmpq%d" % i) for i in range(2)]
            self.bst = cv([128, 4 * 6 + 8], F32)
            self.bst_tk = Tk("bst")
        else:
            self.fgb = cv([128, D], F32)
            self.fgb_tk = Tk("fgb")
            self.S.op("sp", lambda e: e.dma_start(out=self.fgb, in_=self.fng.partition_broadcast(128)),
                      writes=[self.fgb_tk], dma=True)

    def alloc_b1(self):
        cv = self.carve
        self.KT = [cv([128, 2, 2 * NTOK], BF16) for i in range(2)]
        self.KT_tk = [Tk("KT%d" % i) for i in range(2)]
        self.Vb = [cv([128, 32, 264], BF16) for i in range(2)]
        self.Vb_tk = [Tk("Vb%d" % i) for i in range(2)]
        self.Qb = [cv([128, 2, NTOK], BF16) for i in range(2)]
        self.Qb_tk = [Tk("Qb%d" % i) for i in range(2)]
        self.OTh = [cv([128, 2, NTOK], BF16) for i in range(2)]
        self.OTh_tk = [Tk("OTh%d" % i) for i in range(2)]
        self.pT = [cv([128, 2, 128], BF16) for i in range(4)]
        self.pT_tk = [Tk("pT%d" % i) for i in range(4)]
        self.osb = [cv([128, 256], F32) for i in range(2)]
        self.osb_tk = [Tk("osb%d" % i) for i in range(2)]
        self.t1 = [cv([128, 256], F32) for i in range(2)]
        self.t1_tk = [Tk("t1%d" % i) for i in range(2)]
        self.onb = [cv([128, 256], BF16) for i in range(2)]
        self.onb_tk = [Tk("onb%d" % i) for i in range(2)]
        self.jb = cv([128, 256], BF16)
        self.jb_tk = Tk("jb")
        self.ust = [cv([128, 8], F32) for i in range(4)]
        self.ust_tk = [Tk("ust%d" % i) for i in range(4)]
        self.lt = cv([128, 4, 128], F32)
        self.lt_tk = Tk("lt")

    def consts(self):
        S = self.S
        S.op("sp", lambda e: e.dma_start(out=self.pc, in_=self.pcol[:, :]), writes=[self.pc_tk], dma=True)

        def mk_ident(e):
            e.memset(self.ident_f, 0.0)
            return e.affine_select(out=self.ident_f, in_=self.ident_f, pattern=[[-1, 128]],
                                   compare_op=ALU.not_equal, fill=1.0, base=0, channel_multiplier=1)
        S.op("pool", mk_ident, writes=[self.ident_tk])
        S.op("dve", lambda e: e.tensor_copy(out=self.ident, in_=self.ident_f), reads=[self.ident_tk],
             writes=[self.ident_tk])
        S.op("dve", lambda e: e.memset(self.ones, 1.0), writes=[self.ident_tk])

    def rstd_small(self, ss, rs, inv_n, tk):
        S = self.S
        S.op("dve", lambda e: e.tensor_scalar(out=rs, in0=ss, scalar1=inv_n, scalar2=EPS, op0=ALU.mult, op1=ALU.add),
             reads=[tk], writes=[tk])
        S.op("dve", lambda e: e.reciprocal(out=rs, in_=rs), reads=[tk], writes=[tk])
        S.op("act", lambda e: e.activation(out=rs, in_=rs, func=AF.Sqrt), reads=[tk], writes=[tk])

    def wload(self, dst, dst_tk, src):
        self.S.op("pool", lambda e: e.dma_start(out=dst, in_=src), writes=[dst_tk], dma=True)

    def load_x_tile(self, tt, src, row0):
        for s in range(4):
            r0 = row0 + tt * T + s * 128
            self.S.op("sp", lambda e, s=s, r0=r0: e.dma_start(out=self.x_sb[:, s, :], in_=src[r0:r0 + 128, :]),
                      writes=[self.x_tk[s]], dma=True)

    def store_x_tile(self, tt, dst):
        for s in range(4):
            r0 = tt * T + s * 128
            self.S.op("sp", lambda e, s=s, r0=r0: e.dma_start(out=dst[r0:r0 + 128, :], in_=self.x_sb[:, s, :]),
                      reads=[self.x_tk[s]], dma=True)

    def transpose_scale(self, src_fn, src_tk, nblk, gcol0, dst_fn, dst_tk, nrows=128, lo=0, hi=8):
        S = self.S
        for kg in range(0, nblk, 4):
            bk, bk_tk = self.bank(lo, hi)
            bkb = bk.bitcast(BF16)
            n4 = min(4, nblk - kg)
            for kk in range(n4):
                k = kg + kk
                S.op("pe", lambda e, k=k, kk=kk, bkb=bkb: e.transpose(
                    out=bkb[:, kk * 128:kk * 128 + nrows], in_=src_fn(k), identity=self.ident[0:nrows, 0:nrows]),
                    reads=[src_tk, self.ident_tk], writes=[bk_tk])
            for kk in range(n4):
                k = kg + kk
                gc = self.pc[:, gcol0 + k:gcol0 + k + 1]
                dst = dst_fn(k)
                src = bkb[:, kk * 128:kk * 128 + nrows]
                if kk % 2 == 0:
                    S.op("act", lambda e, dst=dst, src=src, gc=gc: e.activation(out=dst, in_=src, func=AF.Copy, scale=gc),
                         reads=[bk_tk, self.pc_tk], writes=[dst_tk])
                else:
                    S.op("dve", lambda e, dst=dst, src=src, gc=gc: e.tensor_scalar(out=dst, in0=src, scalar1=gc,
                                                                                   scalar2=None, op0=ALU.mult),
                         reads=[bk_tk, self.pc_tk], writes=[dst_tk])

    def rmsnorm_T(self, gcol0):
        S = self.S
        xn, xn_tk = self.xn[0], self.xn_tk[0]
        for s in range(4):
            ss = self.stat[:, s:s + 1]
            S.op("act", lambda e, s=s, ss=ss: e.activation(out=xn, in_=self.x_sb[:, s, :], func=AF.Square, accum_out=ss),
                 reads=[self.x_tk[s]], writes=[xn_tk, self.stat_tk])
        self.rstd_small(self.stat[:, 0:4], self.stat[:, 8:12], 1.0 / D, self.stat_tk)
        for s in range(4):
            rs = self.stat[:, 8 + s:9 + s]
            S.op("dve", lambda e, s=s, rs=rs: e.tensor_scalar(out=xn, in0=self.x_sb[:, s, :], scalar1=rs,
                                                              scalar2=None, op0=ALU.mult),
                 reads=[self.x_tk[s], self.stat_tk], writes=[xn_tk])
            self.transpose_scale(lambda k: xn[:, k * 128:(k + 1) * 128], xn_tk, KC, gcol0,
                                 lambda k, s=s: self.hT[:, k, s * 128:(s + 1) * 128], self.hT_tk[s])

    def paired_fm(self, W3, colA, colB, nch, funcB, out_fn, halo_out_fn=None):
        S = self.S
        for sl in range(nch // 2):
            wa = self.nxt("wfm", len(self.wfm))
            self.wload(self.wfm[wa], self.wfm_tk[wa], W3[:, :, colA + sl * 256:colA + (sl + 1) * 256])
            wb = self.nxt("wfm", len(self.wfm))
            self.wload(self.wfm[wb], self.wfm_tk[wb], W3[:, :, colB + sl * 256:colB + (sl + 1) * 256])
            for j in range(2):
                c = sl * 2 + j
                ba, ba_tk = self.bank()
                bb, bb_tk = self.bank()
                for (wi, bk, bk_tk) in ((wa, ba, ba_tk), (wb, bb, bb_tk)):
                    for k in range(KC):
                        S.op("pe", lambda e, wi=wi, bk=bk, k=k, j=j: e.matmul(
                            bk, lhsT=self.wfm[wi][:, k, j * 128:(j + 1) * 128], rhs=self.hT[:, k, :],
                            start=(k == 0), stop=(k == KC - 1)),
                            reads=[self.wfm_tk[wi]] + self.hT_tk, writes=[bk_tk])
                si = self.nxt("sg", 2)
                sg, sg_tk = self.sg[si], self.sg_tk[si]
                S.op("act", lambda e, sg=sg, bb=bb: e.activation(out=sg, in_=bb, func=funcB),
                     reads=[bb_tk], writes=[sg_tk])
                dst, dst_tk = out_fn(c)
                S.op("dve", lambda e, sg=sg, ba=ba, dst=dst: e.tensor_tensor(out=dst, in0=sg, in1=ba, op=ALU.mult),
                     reads=[sg_tk, ba_tk], writes=[dst_tk])
                if halo_out_fn is not None:
                    bh, bh_tk = self.bank()
                    for (wi, off) in ((wa, 0), (wb, HALO)):
                        for k in range(KC):
                            S.op("pe", lambda e, wi=wi, bh=bh, k=k, j=j, off=off: e.matmul(
                                bh[:, off:off + HALO], lhsT=self.wfm[wi][:, k, j * 128:(j + 1) * 128],
                                rhs=self.hTh[:, k, :], start=(k == 0), stop=(k == KC - 1)),
                                reads=[self.wfm_tk[wi], self.hTh_tk], writes=[bh_tk])
                    si = self.nxt("sg", 2)
                    sg, sg_tk = self.sg[si], self.sg_tk[si]
                    S.op("act", lambda e, sg=sg, bh=bh: e.activation(out=sg[:, 0:HALO], in_=bh[:, HALO:2 * HALO], func=funcB),
                         reads=[bh_tk], writes=[sg_tk])
                    dst, dst_tk = halo_out_fn(c)
                    S.op("dve", lambda e, sg=sg, bh=bh, dst=dst: e.tensor_tensor(out=dst, in0=sg[:, 0:HALO],
                                                                               in1=bh[:, 0:HALO], op=ALU.mult),
                         reads=[sg_tk, bh_tk], writes=[dst_tk])

    def fm_evac(self, W3, col0, nch, evac):
        S = self.S
        for sl in range(nch // 2):
            wa = self.nxt("wfm", len(self.wfm))
            self.wload(self.wfm[wa], self.wfm_tk[wa], W3[:, :, col0 + sl * 256:col0 + (sl + 1) * 256])
            for j in range(2):
                c = sl * 2 + j
                ba, ba_tk = self.bank()
                for k in range(KC):
                    S.op("pe", lambda e, wa=wa, ba=ba, k=k, j=j: e.matmul(
                        ba, lhsT=self.wfm[wa][:, k, j * 128:(j + 1) * 128], rhs=self.hT[:, k, :],
                        start=(k == 0), stop=(k == KC - 1)),
                        reads=[self.wfm_tk[wa]] + self.hT_tk, writes=[ba_tk])
                evac(c, ba, ba_tk)

    def tm_lin(self, W3, nk, lhs_fn, col0, consume):
        S = self.S
        for ns in range(4):
            half = self.nxt("tmhalf", 2)
            bks = [(self.banks[half * 4 + s], self.bank_tk[half * 4 + s]) for s in range(4)]
            for kg in range(nk // 4):
                wi = self.nxt("wtm", len(self.wtm))
                self.wload(self.wtm[wi], self.wtm_tk[wi],
                           W3[:, kg * 4:(kg + 1) * 4, col0 + ns * 512:col0 + (ns + 1) * 512])
                for kk in range(4):
                    k = kg * 4 + kk
                    for s in range(4):
                        lap, ltk = lhs_fn(k, s)
                        S.op("pe", lambda e, wi=wi, kk=kk, k=k, bk=bks[s][0], lap=lap: e.matmul(
                            bk, lhsT=lap, rhs=self.wtm[wi][:, kk, :], start=(k == 0), stop=(k == nk - 1)),
                            reads=[self.wtm_tk[wi], ltk], writes=[bks[s][1]])
            for s in range(4):
                consume(s, ns, bks[s][0], bks[s][1])

    def resid_consume(self, s, ns, bk, bk_tk):
        xs = self.x_sb[:, s, ns * 512:(ns + 1) * 512]
        self.S.op("dve", lambda e: e.tensor_tensor(out=xs, in0=xs, in1=bk, op=ALU.add),
                  reads=[bk_tk], writes=[self.x_tk[s]])

    def ffn(self, layer):
        wgu = self.w_gate_up[layer].rearrange("(k p) n -> p k n", p=128)
        wdn = self.w_down[layer].rearrange("(k p) n -> p k n", p=128)
        self.paired_fm(wgu, FH, 0, FC, AF.Silu, lambda c: (self.region[:, c, :], self.reg_tk[c]))
        self.tm_lin(wdn, FC, lambda k, s: (self.region[:, k, s * 128:(s + 1) * 128], self.reg_tk[k]), 0,
                    self.resid_consume)

    def setup_a(self):
        S = self.S
        col = self.col
        WT2 = self.WT.rearrange("p a b -> p (a b)")
        self.wload(WT2, self.WT_tk, self.wsT[:, :])
        S.op("dve", lambda e: e.memset(self.WT[64:128, :, 0:64], 0.0), reads=[self.WT_tk], writes=[self.WT_tk])
        bsb = self.region[:, 40:44, :].bitcast(F32).rearrange("p a b -> p (a b)")
        rbc = self.region[:, 44:48, :].bitcast(F32).rearrange("p a b -> p (a b)")
        tks = self.reg_tk[40:48]
        S.op("sp", lambda e: e.dma_start(out=bsb, in_=self.bs.partition_broadcast(128)), writes=tks[0:4], dma=True)
        for hh in range(2):
            bk, bk_tk = self.bank()
            S.op("pe", lambda e, bk=bk, hh=hh: e.matmul(bk, lhsT=self.ones, rhs=WT2[:, hh * 512:(hh + 1) * 512],
                                                        start=True, stop=True),
                 reads=[self.WT_tk, self.ident_tk], writes=[bk_tk])
            S.op("dve", lambda e, bk=bk, hh=hh: e.tensor_copy(out=rbc[:, hh * 512:(hh + 1) * 512], in_=bk),
                 reads=[bk_tk], writes=tks[4:8])
        lnb0 = col["gmlp_ln_b"]
        for c in range(16):
            g = c // 2
            S.op("dve", lambda e, c=c, g=g: e.scalar_tensor_tensor(
                out=self.Qg[:, c, :], in0=rbc[:, g * 128:(g + 1) * 128], scalar=self.pc[:, lnb0 + c:lnb0 + c + 1],
                in1=bsb[:, g * 128:(g + 1) * 128], op0=ALU.mult, op1=ALU.add),
                reads=tks + [self.pc_tk], writes=[self.Qg_tk])
        xh = self.x_sb[0:HALO, 0, :]
        xnh = self.xn[0][0:HALO, :]
        S.op("sp", lambda e: e.dma_start(out=xh, in_=self.xin[0:HALO, :]), writes=[self.x_tk[0]], dma=True)
        ss = self.stat[0:HALO, 16:17]
        rs = self.stat[0:HALO, 17:18]
        S.op("act", lambda e: e.activation(out=xnh, in_=xh, func=AF.Square, accum_out=ss),
             reads=[self.x_tk[0]], writes=[self.xn_tk[0], self.stat_tk])
        self.rstd_small(ss, rs, 1.0 / D, self.stat_tk)
        S.op("dve", lambda e: e.tensor_scalar(out=xnh, in0=xh, scalar1=rs, scalar2=None, op0=ALU.mult),
             reads=[self.x_tk[0], self.stat_tk], writes=[self.xn_tk[0]])
        self.transpose_scale(lambda k: xnh[:, k * 128:(k + 1) * 128], self.xn_tk[0], KC, col["mix_norm_g0"],
                             lambda k: self.hTh[:, k, :], self.hTh_tk, nrows=HALO)

    def layer0_tile(self, tt):
        S = self.S
        col = self.col
        w_in3 = self.w_in.rearrange("(k p) n -> p k n", p=128)
        w_out3 = self.w_out.rearrange("(k p) n -> p k n", p=128)
        w_qkv3 = self.w_qkv.rearrange("(k p) n -> p k n", p=128)
        R = self.region
        self.load_x_tile(tt, self.xin, HALO)
        self.rmsnorm_T(col["mix_norm_g0"])
        if tt > 0:
            S.op("dve", lambda e: e.tensor_copy(out=self.gluT[:, :, 0:HALO], in_=self.gluT[:, :, T:T + HALO]),
                 reads=self.glu_tk, writes=self.glu_tk)
        self.paired_fm(w_in3, 0, 2048, 16, AF.Sigmoid,
                       lambda c: (self.gluT[:, c, HALO:HALO + T], self.glu_tk[c]),
                       halo_out_fn=(lambda c: (self.gluT[:, c, 0:HALO], self.glu_tk[c])) if tt == 0 else None)
        def evac_u(c, bk, bk_tk):
            S.op("act", lambda e: e.activation(out=R[:, 32 + c, :], in_=bk, func=AF.Gelu),
                 reads=[bk_tk], writes=[self.reg_tk[32 + c]])
        self.fm_evac(w_in3, 4096, 16, evac_u)

        def evac_v(s, ns, bk, bk_tk):
            S.op("act", lambda e: e.activation(out=R[:, s * 4 + ns, :], in_=bk, func=AF.Gelu),
                 reads=[bk_tk], writes=[self.reg_tk[s * 4 + ns]])
        self.tm_lin(w_in3, KC, lambda k, s: (self.hT[:, k, s * 128:(s + 1) * 128], self.hT_tk[s]), 6144, evac_v)
        lng0 = col["gmlp_ln_g"]
        for s in range(4):
            vt = self.reg_tk[s * 4:s * 4 + 4]
            for ns in range(4):
                S.op("dve", lambda e, s=s, ns=ns: e.bn_stats(out=self.bst[:, ns * 6:(ns + 1) * 6], in_=R[:, s * 4 + ns, :]),
                     reads=[vt[ns]], writes=[self.bst_tk])
            mv = self.bst[:, 24:26]
            S.op("dve", lambda e, mv=mv: e.bn_aggr(out=mv, in_=self.bst[:, 0:24]), reads=[self.bst_tk], writes=[self.bst_tk])
            rs = self.bst[:, 26:27]
            nmr = self.bst[:, 27:28]
            self.rstd_small(self.bst[:, 25:26], rs, 1.0, self.bst_tk)
            S.op("dve", lambda e, rs=rs, nmr=nmr: e.scalar_tensor_tensor(out=nmr, in0=self.bst[:, 24:25], scalar=-1.0, in1=rs,
                                                                        op0=ALU.mult, op1=ALU.mult),
                 reads=[self.bst_tk], writes=[self.bst_tk])
            S.op("act", lambda e, s=s, rs=rs, nmr=nmr: e.activation(out=R[:, s * 4:s * 4 + 4, :], in_=R[:, s * 4:s * 4 + 4, :],
                                                                   func=AF.Identity, scale=rs, bias=nmr),
                 reads=[self.bst_tk] + vt, writes=vt)
            for cg in range(4):
                bk, bk_tk = self.bank()
                for cc in range(4):
                    c = cg * 4 + cc
                    S.op("pe", lambda e, bk=bk, cc=cc, c=c, s=s: e.matmul(
                        bk[:, cc * 128:(cc + 1) * 128], lhsT=R[:, s * 4 + c // 4, (c % 4) * 128:(c % 4 + 1) * 128],
                        rhs=self.WT[:, c // 2, :], start=True, stop=True),
                        reads=[self.reg_tk[s * 4 + c // 4], self.WT_tk], writes=[bk_tk])
                for cc in range(4):
                    c = cg * 4 + cc
                    qi = self.nxt("tmpq", 2)
                    tq, tq_tk = self.tmpq[qi], self.tmpq_tk[qi]
                    S.op("dve", lambda e, bk=bk, cc=cc, c=c, tq=tq: e.scalar_tensor_tensor(
                        out=tq, in0=bk[:, cc * 128:(cc + 1) * 128], scalar=self.pc[:, lng0 + c:lng0 + c + 1],
                        in1=self.Qg[:, c, :], op0=ALU.mult, op1=ALU.add),
                        reads=[bk_tk, self.Qg_tk, self.pc_tk], writes=[tq_tk])
                    S.op("dve", lambda e, c=c, s=s, tq=tq: e.tensor_tensor(
                        out=R[:, 16 + c, s * 128:(s + 1) * 128], in0=tq, in1=R[:, 32 + c, s * 128:(s + 1) * 128], op=ALU.mult),
                        reads=[tq_tk, self.reg_tk[32 + c]], writes=[self.reg_tk[16 + c]])
        cw0, cb0 = col["conv_w"], col["conv_b"]
        for c in range(16):
            di = self.nxt("dg", 2)
            dg, dg_tk = self.dg[di], self.dg_tk[di]
            S.op("dve", lambda e, dg=dg, c=c: e.tensor_tensor(
                out=dg, in0=self.ident.unsqueeze(1).broadcast_to([128, 31, 128]),
                in1=self.pc[:, cw0 + c * 31:cw0 + (c + 1) * 31].unsqueeze(2).broadcast_to([128, 31, 128]), op=ALU.mult),
                reads=[self.ident_tk, self.pc_tk], writes=[dg_tk])
            bk, bk_tk = self.bank()
            for j in range(31):
                S.op("pe", lambda e, bk=bk, dg=dg, j=j, c=c: e.matmul(
                    bk, lhsT=dg[:, j, :], rhs=self.gluT[:, c, 2 + j:2 + j + T], start=(j == 0), stop=(j == 30)),
                    reads=[dg_tk, self.glu_tk[c]], writes=[bk_tk])
            bias = self.pc[:, cb0 + c:cb0 + c + 1]
            S.op("act", lambda e, bk=bk, c=c, bias=bias: e.activation(out=R[:, c, :], in_=bk, func=AF.Identity, bias=bias),
                 reads=[bk_tk, self.pc_tk], writes=[self.reg_tk[c]])
            S.op("act", lambda e, bk=bk, c=c, bias=bias: e.activation(out=R[:, 32 + c, :], in_=bk, func=AF.Square, bias=bias),
                 reads=[bk_tk, self.pc_tk], writes=[self.reg_tk[32 + c]])
        b1, b1_tk = self.bank()
        b2, b2_tk = self.bank()
        for (bk, bk_tk, u0) in ((b1, b1_tk, 0), (b2, b2_tk, 32)):
            for c in range(16):
                S.op("pe", lambda e, bk=bk, u0=u0, c=c: e.matmul(bk, lhsT=self.ones, rhs=R[:, u0 + c, :],
                                                                 start=(c == 0), stop=(c == 15)),
                     reads=[self.reg_tk[u0 + c], self.ident_tk], writes=[bk_tk])
        mean, rstd, nmr = self.cst[:, 0, :], self.cst[:, 1, :], self.cst[:, 2, :]
        ctk = self.cst_tk
        S.op("dve", lambda e: e.tensor_scalar(out=mean, in0=b1, scalar1=1.0 / D, scalar2=None, op0=ALU.mult),
             reads=[b1_tk], writes=[ctk])
        S.op("dve", lambda e: e.tensor_tensor(out=nmr, in0=mean, in1=mean, op=ALU.mult), reads=[ctk], writes=[ctk])
        S.op("dve", lambda e: e.scalar_tensor_tensor(out=rstd, in0=b2, scalar=1.0 / D, in1=nmr, op0=ALU.mult,
                                                     op1=ALU.subtract),
             reads=[b2_tk, ctk], writes=[ctk])
        S.op("dve", lambda e: e.tensor_scalar(out=rstd, in0=rstd, scalar1=EPS, scalar2=None, op0=ALU.add),
             reads=[ctk], writes=[ctk])
        S.op("dve", lambda e: e.reciprocal(out=rstd, in_=rstd), reads=[ctk], writes=[ctk])
        S.op("act", lambda e: e.activation(out=rstd, in_=rstd, func=AF.Sqrt), reads=[ctk], writes=[ctk])
        S.op("dve", lambda e: e.scalar_tensor_tensor(out=nmr, in0=mean, scalar=-1.0, in1=rstd, op0=ALU.mult, op1=ALU.mult),
             reads=[ctk], writes=[ctk])
        clg, clb = col["conv_ln_g"], col["conv_ln_b"]
        for c in range(16):
            si = self.nxt("sg", 2)
            sg, sg_tk = self.sg[si], self.sg_tk[si]
            S.op("dve", lambda e, c=c, sg=sg: e.tensor_tensor(out=sg, in0=R[:, c, :], in1=rstd, op=ALU.mult),
                 reads=[self.reg_tk[c], ctk], writes=[sg_tk])
            S.op("dve", lambda e, sg=sg: e.tensor_tensor(out=sg, in0=sg, in1=nmr, op=ALU.add),
                 reads=[ctk], writes=[sg_tk])
            S.op("act", lambda e, c=c, sg=sg: e.activation(out=R[:, c, :], in_=sg, func=AF.Silu,
                                                           scale=self.pc[:, clg + c:clg + c + 1],
                                                           bias=self.pc[:, clb + c:clb + c + 1]),
                 reads=[sg_tk, self.pc_tk], writes=[self.reg_tk[c]])
        self.tm_lin(w_out3, 32, lambda k, s: (R[:, k, s * 128:(s + 1) * 128], self.reg_tk[k]), 0, self.resid_consume)
        self.rmsnorm_T(col["ffn_norm_g0"])
        self.ffn(0)
        self.store_x_tile(tt, self.x1s)
        self.rmsnorm_T(col["mix_norm_g1"])
        for (col0, u0, dstT) in ((0, 0, self.qTs), (2048, 16, self.kTs)):
            def evac_qk(c, bk, bk_tk, u0=u0, dstT=dstT):
                eng = "act" if c % 2 == 0 else "dve"
                if eng == "act":
                    S.op("act", lambda e: e.activation(out=R[:, u0 + c, :], in_=bk, func=AF.Copy),
                         reads=[bk_tk], writes=[self.reg_tk[u0 + c]])
                else:
                    S.op("dve", lambda e: e.tensor_copy(out=R[:, u0 + c, :], in_=bk),
                         reads=[bk_tk], writes=[self.reg_tk[u0 + c]])
                S.op("sp", lambda e: e.dma_start(out=dstT[c][:, tt * T:(tt + 1) * T], in_=R[:, u0 + c, :]),
                     reads=[self.reg_tk[u0 + c]], dma=True)
            self.fm_evac(w_qkv3, col0, 16, evac_qk)

        def evac_vv(s, ns, bk, bk_tk):
            u = 32 + s * 4 + ns
            if ns % 2 == 0:
                S.op("act", lambda e: e.activation(out=R[:, u, :], in_=bk, func=AF.Copy), reads=[bk_tk], writes=[self.reg_tk[u]])
            else:
                S.op("dve", lambda e: e.tensor_copy(out=R[:, u, :], in_=bk), reads=[bk_tk], writes=[self.reg_tk[u]])
        self.tm_lin(w_qkv3, KC, lambda k, s: (self.hT[:, k, s * 128:(s + 1) * 128], self.hT_tk[s]), 4096, evac_vv)
        for s in range(4):
            r0 = tt * T + s * 128
            S.op("sp", lambda e, s=s, r0=r0: e.dma_start(
                out=self.vs[r0:r0 + 128, :].rearrange("p (a b) -> p a b", a=4), in_=R[:, 32 + s * 4:32 + s * 4 + 4, :]),
                reads=self.reg_tk[32 + s * 4:32 + s * 4 + 4], dma=True)

    def setup_b(self):
        S = self.S
        m = self.misc
        mt = self.misc_tk
        lt2 = self.lt.rearrange("p a b -> p (a b)")
        S.op("sp", lambda e: e.dma_start(out=lt2, in_=self.lamv.rearrange("a b -> (a b)").partition_broadcast(128)),
             writes=[self.lt_tk], dma=True)
        for i in range(2):
            S.op("dve", lambda e, i=i: e.tensor_tensor(out=self.lt[:, 2 * i, :], in0=self.lt[:, 2 * i, :],
                                                       in1=self.lt[:, 2 * i + 1, :], op=ALU.mult),
                 reads=[self.lt_tk], writes=[self.lt_tk])
            S.op("act", lambda e, i=i: e.activation(out=self.lt[:, 2 * i + 1, :], in_=self.lt[:, 2 * i, :], func=AF.Copy,
                                                    accum_out=m[:, 5 + i:6 + i]),
                 reads=[self.lt_tk], writes=[self.lt_tk, mt])
        S.op("act", lambda e: e.activation(out=m[:, 5:7], in_=m[:, 5:7], func=AF.Exp), reads=[mt], writes=[mt])
        S.op("dve", lambda e: e.tensor_tensor(out=m[:, 0:1], in0=m[:, 6:7], in1=m[:, 5:6], op=ALU.subtract),
             reads=[mt], writes=[mt])
        S.op("dve", lambda e: e.tensor_scalar(out=m[:, 0:1], in0=m[:, 0:1], scalar1=-LAMBDA_INIT, scalar2=None, op0=ALU.add),
             reads=[mt], writes=[mt])
        S.op("dve", lambda e: e.memset(m[:, 1:2], 0.0), reads=[mt], writes=[mt])
        S.op("dve", lambda e: e.memset(m[64:128, 1:2], NEG), reads=[mt], writes=[mt])
        S.op("sp", lambda e: e.dma_start(out=m[:, 2:3], in_=self.rbias_in[:, :]), reads=[mt], writes=[mt], dma=True)
        sc = self.col["subln"]
        S.op("dve", lambda e: e.tensor_scalar(out=m[:, 3:5], in0=self.pc[:, sc:sc + 2], scalar1=1.0 - LAMBDA_INIT,
                                              scalar2=None, op0=ALU.mult),
             reads=[mt, self.pc_tk], writes=[mt])
        for b in range(2):
            S.op("dve", lambda e, b=b: e.memset(self.Vb[b][:, :, 256:257], 1.0), writes=[self.Vb_tk[b]])

    def attention(self):
        for h in range(8):
            self.attn_head(h)

    def attn_head(self, h):
        S = self.S
        b = h % 2
        KT, KT_tk = self.KT[b], self.KT_tk[b]
        Vb, Vb_tk = self.Vb[b], self.Vb_tk[b]
        Qb, Qb_tk = self.Qb[b], self.Qb_tk[b]
        OT, OT_tk = self.OTh[b], self.OTh_tk[b]
        for c in range(2):
            n = 2 * h + c
            S.op("sp", lambda e, n=n, c=c: e.dma_start(out=KT[:, c, 0:NTOK], in_=self.kTr[n]),
                 writes=[KT_tk], dma=True)
            S.op("sp", lambda e, n=n, c=c: e.dma_start(out=KT[:, c, NTOK:2 * NTOK], in_=self.kTs[n]),
                 writes=[KT_tk], dma=True)
            S.op("sp", lambda e, n=n, c=c: e.dma_start(out=Qb[:, c, :], in_=self.qTs[n]),
                 writes=[Qb_tk], dma=True)
        for (src, k0) in ((self.vr, 0), (self.vs, 16)):
            S.op("sp", lambda e, src=src, k0=k0: e.dma_start(
                out=Vb[:, k0:k0 + 16, 0:256],
                in_=src[:, h * 256:(h + 1) * 256].rearrange("(kb p) e -> p kb e", p=128)),
                writes=[Vb_tk], dma=True)
        for qs in range(16):
            self.attn_unit(h, qs, KT, KT_tk, Vb, Vb_tk, Qb, Qb_tk, OT, OT_tk)
        for hf in range(2):
            S.op("sp", lambda e, hf=hf: e.dma_start(out=self.oTs[2 * h + hf], in_=OT[:, hf, :]),
                 reads=[OT_tk], dma=True)

    def attn_unit(self, h, qs, KT, KT_tk, Vb, Vb_tk, Qb, Qb_tk, OT, OT_tk):
        S = self.S
        m = self.misc
        mt = self.misc_tk
        o0, o0_tk = self.bank(0, 4)
        o1, o1_tk = self.bank(0, 4)
        psO = ((o0, o0_tk), (o1, o1_tk))
        steps = [(kb, 0) for kb in range(16)] + [(16 + kb, 1 if kb < qs else 2) for kb in range(qs + 1)]
        nst = len(steps)
        ptl = [None] * nst

        def qk(i):
            kb, kind = steps[i]
            sb_, sb_tk = self.bank(4, 8)
            for c in range(2):
                S.op("pe", lambda e, c=c: e.matmul(
                    sb_[:, c * 128:(c + 1) * 128], lhsT=KT[:, c, kb * 128:(kb + 1) * 128],
                    rhs=Qb[:, c, qs * 128:(qs + 1) * 128], start=True, stop=True),
                    reads=[KT_tk, Qb_tk], writes=[sb_tk])
            pi = self.nxt("pT", 4)
            pT, pT_tk = self.pT[pi], self.pT_tk[pi]
            sv = sb_[:, 0:256].rearrange("p (c q) -> p c q", c=2)
            if kind == 0:
                S.op("act", lambda e: e.activation(out=pT, in_=sv, func=AF.Exp, scale=SCALE, bias=m[:, 2:3]),
                     reads=[sb_tk, mt], writes=[pT_tk])
            elif kind == 1:
                S.op("act", lambda e: e.activation(out=pT, in_=sv, func=AF.Exp, scale=SCALE),
                     reads=[sb_tk], writes=[pT_tk])
            else:
                S.op("act", lambda e: e.activation(out=pT[:, :, 0:64], in_=sv[:, :, 0:64], func=AF.Exp,
                                                   scale=SCALE, bias=m[:, 1:2]),
                     reads=[sb_tk, mt], writes=[pT_tk])
                S.op("act", lambda e: e.activation(out=pT[:, :, 64:128], in_=sv[:, :, 64:128], func=AF.Exp, scale=SCALE),
                     reads=[sb_tk], writes=[pT_tk])
            ptl[i] = (pT, pT_tk)

        def pv(i):
            kb, kind = steps[i]
            pT, pT_tk = ptl[i]
            for c in range(2):
                S.op("pe", lambda e, c=c: e.matmul(
                    psO[c][0][:, 0:257], lhsT=pT[:, c, :], rhs=Vb[:, kb, 0:257],
                    start=(i == 0), stop=(i == nst - 1)),
                    reads=[pT_tk, Vb_tk], writes=[psO[c][1]])

        LA = 2
        for i in range(nst + LA):
            if i < nst:
                qk(i)
            if i >= LA:
                pv(i - LA)
        ui = self.nxt("ust", 4)
        us, us_tk = self.ust[ui], self.ust_tk[ui]
        oi = self.nxt("osb", 2)
        osb, osb_tk = self.osb[oi], self.osb_tk[oi]
        t1, t1_tk = self.t1[oi], self.t1_tk[oi]
        onb, onb_tk = self.onb[oi], self.onb_tk[oi]
        S.op("dve", lambda e: e.reciprocal(out=us[:, 0:1], in_=o0[:, 256:257]), reads=[o0_tk], writes=[us_tk])
        S.op("dve", lambda e: e.reciprocal(out=us[:, 1:2], in_=o1[:, 256:257]), reads=[o1_tk], writes=[us_tk])
        S.op("dve", lambda e: e.tensor_scalar(out=t1, in0=o1[:, 0:256], scalar1=us[:, 1:2], scalar2=m[:, 0:1],
                                              op0=ALU.mult, op1=ALU.mult),
             reads=[o1_tk, us_tk, mt], writes=[t1_tk])
        S.op("dve", lambda e: e.scalar_tensor_tensor(out=osb, in0=o0[:, 0:256], scalar=us[:, 0:1], in1=t1,
                                                     op0=ALU.mult, op1=ALU.add),
             reads=[o0_tk, us_tk, t1_tk], writes=[osb_tk])
        S.op("act", lambda e: e.activation(out=self.jb, in_=osb, func=AF.Square, accum_out=us[:, 2:3]),
             reads=[osb_tk], writes=[self.jb_tk, us_tk])
        self.rstd_small(us[:, 2:3], us[:, 3:4], 1.0 / 256, us_tk)
        S.op("act", lambda e: e.activation(out=onb, in_=osb, func=AF.Copy, scale=us[:, 3:4]),
             reads=[osb_tk, us_tk], writes=[onb_tk])
        bk, bk_tk = self.bank(4, 8)
        bkb = bk.bitcast(BF16)
        for hf in range(2):
            S.op("pe", lambda e, hf=hf: e.transpose(
                out=bkb[:, hf * 128:(hf + 1) * 128], in_=onb[:, hf * 128:(hf + 1) * 128], identity=self.ident),
                reads=[onb_tk, self.ident_tk], writes=[bk_tk])
        S.op("act", lambda e: e.activation(out=OT[:, 0, qs * 128:(qs + 1) * 128], in_=bkb[:, 0:128],
                                           func=AF.Copy, scale=m[:, 3:4]),
             reads=[bk_tk, mt], writes=[OT_tk])
        S.op("dve", lambda e: e.tensor_scalar(out=OT[:, 1, qs * 128:(qs + 1) * 128], in0=bkb[:, 128:256],
                                              scalar1=m[:, 4:5], scalar2=None, op0=ALU.mult),
             reads=[bk_tk, mt], writes=[OT_tk])

    def layer1_tile(self, tt):
        S = self.S
        col = self.col
        R = self.region
        w_o3 = self.w_o.rearrange("(k p) n -> p k n", p=128)
        self.load_x_tile(tt, self.x1s, 0)
        for c in range(16):
            S.op("sp", lambda e, c=c: e.dma_start(out=R[:, c, :], in_=self.oTs[c][:, tt * T:(tt + 1) * T]),
                 writes=[self.reg_tk[c]], dma=True)
        self.tm_lin(w_o3, KC, lambda k, s: (R[:, k, s * 128:(s + 1) * 128], self.reg_tk[k]), 0, self.resid_consume)
        self.rmsnorm_T(col["ffn_norm_g1"])
        self.ffn(1)
        xn, xn_tk = self.xn[0], self.xn_tk[0]
        for s in range(4):
            ss = self.stat[:, s:s + 1]
            S.op("act", lambda e, s=s, ss=ss: e.activation(out=xn, in_=self.x_sb[:, s, :], func=AF.Square, accum_out=ss),
                 reads=[self.x_tk[s]], writes=[xn_tk, self.stat_tk])
        self.rstd_small(self.stat[:, 0:4], self.stat[:, 8:12], 1.0 / D, self.stat_tk)
        for s in range(4):
            rs = self.stat[:, 8 + s:9 + s]
            S.op("dve", lambda e, s=s, rs=rs: e.scalar_tensor_tensor(out=self.x_sb[:, s, :], in0=self.x_sb[:, s, :], scalar=rs,
                                                                    in1=self.fgb, op0=ALU.mult, op1=ALU.mult),
                 reads=[self.stat_tk, self.fgb_tk], writes=[self.x_tk[s]])
        self.store_x_tile(tt, self.out)

    def dbg(self, name, ap, tks, shape, dt=F32):
        if not self.cfg.get("debug"):
            return
        d = self.dout(name, shape, dt)
        self.S.op("sp", lambda e: e.dma_start(out=d, in_=ap), reads=tks, dma=True)

    def exchange(self):
        raise NotImplementedError


def _cols(vec):
    return np.ascontiguousarray(np.asarray(vec, np.float32).reshape(-1, 128).T)


def make_pcol(inp):
    cols = {}
    parts = []
    n = 0

    def add(name, arr):
        nonlocal n
        cols[name] = n
        parts.append(np.asarray(arr, np.float32))
        n += arr.shape[1]

    for l in range(2):
        add("mix_norm_g%d" % l, _cols(inp["mix_norm_g"][l]))
        add("ffn_norm_g%d" % l, _cols(inp["ffn_norm_g"][l]))
    add("conv_b", _cols(inp["conv_b"][0]))
    add("conv_ln_g", _cols(inp["conv_ln_g"][0]))
    add("conv_ln_b", _cols(inp["conv_ln_b"][0]))
    add("gmlp_ln_g", _cols(inp["gmlp_ln_g"][0]))
    add("gmlp_ln_b", _cols(inp["gmlp_ln_b"][0]))
    add("subln", _cols(inp["subln_g"][0]))
    cw = np.asarray(inp["conv_w"][0], np.float32).reshape(31, 16, 128)
    add("conv_w", np.ascontiguousarray(cw.transpose(2, 1, 0).reshape(128, 16 * 31)))
    return np.ascontiguousarray(np.concatenate(parts, axis=1)), cols


def shard_x(x):
    outs = []
    for core in range(NCORES):
        b, h = core // 2, core % 2
        if h == 0:
            halo = np.zeros((HALO, D), np.float32)
        else:
            halo = x[b, NTOK - HALO:NTOK]
        outs.append(np.ascontiguousarray(np.concatenate([halo, x[b, h * NTOK:(h + 1) * NTOK]], axis=0)))
    return outs


_CACHE = {}


def _get_prog(mode, npcol, cols):
    key = mode
    if key not in _CACHE:
        p = Prog(dict(mode=mode, npcol=npcol, col=cols))
        _CACHE[key] = p.build()
    return _CACHE[key]


def kernel(**inp):
    inp = {k: np.asarray(v) for k, v in inp.items()}
    pcol, cols = make_pcol(inp)
    xs = shard_x(inp["x"])
    wsT = np.ascontiguousarray(inp["gmlp_w_s"][0].transpose(2, 0, 1).reshape(128, 8 * 128))
    bs = np.ascontiguousarray(inp["gmlp_b_s"][0].reshape(1, 8 * 128))
    lamv = np.ascontiguousarray(np.stack([inp["lambda_q1"][0], inp["lambda_k1"][0],
                                          inp["lambda_q2"][0], inp["lambda_k2"][0]]).astype(np.float32))
    fng = np.ascontiguousarray(inp["final_norm_g"].reshape(1, D).astype(np.float32))
    cores = list(range(NCORES))
    common = dict(pcol=pcol, w_gate_up=inp["w_gate_up"], w_down=inp["w_down"])
    ncA = _get_prog("A", pcol.shape[1], cols)
    mapsA = [dict(common, xin=xs[c], w_in=inp["w_in_even"][0], w_out=inp["w_out_even"][0],
                  w_qkv=inp["w_qkv_odd"][0], wsT=wsT, bs=bs) for c in cores]
    rA = run_bass_kernel_spmd(ncA, mapsA, core_ids=cores).results
    ncB = _get_prog("B", pcol.shape[1], cols)
    mapsB = []
    for c in cores:
        first = (c // 2) * 2
        rb = np.full((128, 1), 0.0 if c % 2 == 1 else NEG, np.float32)
        mapsB.append(dict(common, w_o=inp["w_o_odd"][0], lamv=lamv, rbias=rb, fng=fng,
                          x1s=rA[c]["x1s"], qTs=rA[c]["qTs"], kTs=rA[c]["kTs"], vs=rA[c]["vs"],
                          kTr=rA[first]["kTs"], vr=rA[first]["vs"]))
    rB = run_bass_kernel_spmd(ncB, mapsB, core_ids=cores).results
    out = np.empty((4, 2 * NTOK, D), np.float32)
    for c in cores:
        out[c // 2, (c % 2) * NTOK:(c % 2 + 1) * NTOK] = rB[c]["out"]
    return out
```
